# Optimizing a Trainium2 kernel written in Bass

```python
import jax, jax.numpy as jnp
from jax import lax
import numpy as np

D_MODEL = 4096
BATCH = 4
SEQ = 4096
DEPTH = 1

EXPAND = 2
D_MIX = EXPAND * D_MODEL
D_ATTN = D_MIX // 2
D_RNN = D_MIX - D_ATTN
ATTN_HEAD_DIM = 64
N_Q_HEADS = D_ATTN // ATTN_HEAD_DIM
N_KV_HEADS = N_Q_HEADS // 8
GQA_GROUP = N_Q_HEADS // N_KV_HEADS
D_KV = N_KV_HEADS * ATTN_HEAD_DIM
WINDOW = 128
RNN_HEAD_DIM = 128
N_RNN_HEADS = D_RNN // RNN_HEAD_DIM
CHUNK = 64
NORM_EPS = 1e-6

COL_SIZES = [D_ATTN, D_KV, D_KV, D_ATTN, D_RNN, D_RNN, D_RNN, D_RNN]
D_IN = int(sum(COL_SIZES))
SPLIT_POINTS = [int(c) for c in np.cumsum(COL_SIZES)[:-1]]

kernel_name = "hymba_swa_sink_hgrn2_sandwich"


def rms_norm(x, gain):
    xf = x.astype(jnp.float32)
    y = xf * lax.rsqrt(jnp.mean(xf * xf, axis=-1, keepdims=True) + NORM_EPS)
    return (y * gain.astype(jnp.float32)).astype(x.dtype)


def sliding_window_attention(q, k, v, sinks):
    B, S = q.shape[0], q.shape[1]
    nb = S // WINDOW
    q = q.reshape(B, nb, WINDOW, N_KV_HEADS, GQA_GROUP, ATTN_HEAD_DIM)
    k = k.reshape(B, nb, WINDOW, N_KV_HEADS, ATTN_HEAD_DIM)
    v = v.reshape(B, nb, WINDOW, N_KV_HEADS, ATTN_HEAD_DIM)
    pad = ((0, 0), (1, 0), (0, 0), (0, 0), (0, 0))
    kk = jnp.concatenate([jnp.pad(k, pad)[:, :-1], k], axis=2)
    vv = jnp.concatenate([jnp.pad(v, pad)[:, :-1], v], axis=2)
    scale = ATTN_HEAD_DIM ** -0.5
    scores = jnp.einsum('bnqhgd,bnkhd->bnhgqk', q, kk).astype(jnp.float32) * scale
    qi = jnp.arange(WINDOW)[:, None]
    kj = jnp.arange(2 * WINDOW)[None, :]
    band = (kj > qi) & (kj <= qi + WINDOW)
    blk = jnp.arange(nb)[:, None, None]
    valid = band[None] & ((blk > 0) | (kj[None] >= WINDOW))
    scores = jnp.where(valid[None, :, None, None], scores, -jnp.inf)
    sink = sinks.astype(jnp.float32).reshape(N_KV_HEADS, GQA_GROUP)[None, None, :, :, None, None]
    m = jnp.maximum(jnp.max(scores, axis=-1, keepdims=True), sink)
    p = jnp.exp(scores - m)
    denom = jnp.sum(p, axis=-1, keepdims=True) + jnp.exp(sink - m)
    probs = (p / denom).astype(v.dtype)
    out = jnp.einsum('bnhgqk,bnkhd->bnqhgd', probs, vv)
    return out.reshape(B, S, D_ATTN)


def hgrn2_recurrence(q, k, v, g):
    B, S, H, dk = q.shape
    dv = v.shape[-1]
    nc = S // CHUNK

    def to_chunks(t):
        return t.astype(jnp.float32).reshape(B, nc, CHUNK, H, t.shape[-1]).transpose(1, 0, 3, 2, 4)

    qc, kc, vc, gc = to_chunks(q), to_chunks(k), to_chunks(v), to_chunks(g)
    causal = jnp.tril(jnp.ones((CHUNK, CHUNK), dtype=bool))

    def step(state, inp):
        qb, kb, vb, gb = inp
        G = jnp.cumsum(gb, axis=2)
        inter = jnp.einsum('bhtd,bhde->bhte', qb * jnp.exp(G), state)
        diff = G[:, :, :, None, :] - G[:, :, None, :, :]
        decay = jnp.exp(jnp.where(causal[:, :, None], diff, -jnp.inf))
        attn = jnp.einsum('bhtd,bhsd,bhtsd->bhts', qb, kb, decay)
        intra = jnp.einsum('bhts,bhse->bhte', attn, vb)
        G_last = G[:, :, -1:, :]
        new_state = (jnp.exp(G_last[:, :, 0, :])[..., None] * state
                     + jnp.einsum('bhsd,bhse->bhde', kb * jnp.exp(G_last - G), vb))
        return new_state, inter + intra

    init = jnp.zeros((B, H, dk, dv), jnp.float32)
    _, out = lax.scan(step, init, (qc, kc, vc, gc))
    return out.transpose(1, 0, 3, 2, 4).reshape(B, S, H, dv)


def setup_inputs(seed: int = 0) -> dict:
    key = jax.random.key(seed)
    ks = jax.random.split(key, 8)
    x = jax.random.normal(ks[0], (BATCH, SEQ, D_MODEL), jnp.float32)
    w_in = jax.random.normal(ks[1], (DEPTH, D_MODEL, D_IN), jnp.float32) * D_MODEL ** -0.5
    attn_sinks = jax.random.normal(ks[2], (DEPTH, N_Q_HEADS), jnp.float32)
    lb_logits = 0.5 * jax.random.normal(ks[3], (DEPTH + 1, D_RNN), jnp.float32)
    rnn_norm = 1.0 + 0.1 * jax.random.normal(ks[4], (DEPTH, D_RNN), jnp.float32)
    w_out = jax.random.normal(ks[5], (DEPTH, D_MIX, D_MODEL), jnp.float32) * D_MIX ** -0.5
    pre_norm = 1.0 + 0.1 * jax.random.normal(ks[6], (DEPTH, D_MODEL), jnp.float32)
    post_norm = 1.0 + 0.1 * jax.random.normal(ks[7], (DEPTH, D_MODEL), jnp.float32)
    return {"x": x, "w_in": w_in, "attn_sinks": attn_sinks, "lb_logits": lb_logits,
            "rnn_norm": rnn_norm, "w_out": w_out, "pre_norm": pre_norm, "post_norm": post_norm}


def reference(x, w_in, attn_sinks, lb_logits, rnn_norm, w_out, pre_norm, post_norm):
    B, S, _ = x.shape
    lb_table = jnp.cumsum(jax.nn.softmax(lb_logits.astype(jnp.float32), axis=0), axis=0)
    for layer in range(DEPTH):
        h = rms_norm(x, pre_norm[layer])
        proj = jnp.einsum('bsd,de->bse', h, w_in[layer])
        aq, ak, av, ag, rq, rf, ri, rg = jnp.split(proj, SPLIT_POINTS, axis=-1)

        attn = sliding_window_attention(
            aq.reshape(B, S, N_Q_HEADS, ATTN_HEAD_DIM),
            ak.reshape(B, S, N_KV_HEADS, ATTN_HEAD_DIM),
            av.reshape(B, S, N_KV_HEADS, ATTN_HEAD_DIM),
            attn_sinks[layer])
        attn = attn * jax.nn.silu(ag)

        lb = lb_table[layer]
        f = lb + (1.0 - lb) * jax.nn.sigmoid(rf.astype(jnp.float32))
        g = jnp.log(f)
        k = 1.0 - f
        q = jax.nn.silu(rq)
        shp = (B, S, N_RNN_HEADS, RNN_HEAD_DIM)
        o = hgrn2_recurrence(q.reshape(shp), k.reshape(shp), ri.reshape(shp), g.reshape(shp))
        o = rms_norm(o, rnn_norm[layer].reshape(N_RNN_HEADS, RNN_HEAD_DIM))
        o = o.reshape(B, S, D_RNN).astype(x.dtype) * jax.nn.silu(rg)

        mixed = jnp.concatenate([attn, o], axis=-1)
        y = jnp.einsum('bse,ed->bsd', mixed, w_out[layer])
        x = x + rms_norm(y, post_norm[layer])
    return x
```

```python
import numpy as np
from contextlib import ExitStack
import concourse.bass as bass
import concourse.mybir as mybir
from concourse.bass_utils import run_bass_kernel_spmd

F32 = mybir.dt.float32
BF16 = mybir.dt.bfloat16
AF = mybir.ActivationFunctionType
ALU = mybir.AluOpType

NEG = -30000.0
NW = 7
T = 1024
TP = T + 128
OFF = dict(aq=0, ak=4096, av=4608, ag=5120, rq=9216, rf=13312, ri=17408, rg=21504)


class Buf:
    __slots__ = ("name", "w", "r")

    def __init__(self, name):
        self.name = name
        self.w = None
        self.r = {}


class Op:
    __slots__ = ("eng", "fn", "deps", "flag", "idx", "epoch", "dma", "dsem", "dval", "val")

    def __init__(self, eng, fn, epoch, dma=False):
        self.eng = eng
        self.fn = fn
        self.deps = []
        self.flag = False
        self.idx = -1
        self.epoch = epoch
        self.dma = dma
        self.dsem = None
        self.dval = 0
        self.val = 0


class Prog:
    ENGS = ("pe", "act", "dve", "pool", "sp")
    ND = 8

    def __init__(self):
        self.ops = {e: [] for e in self.ENGS}
        self.epoch = 0
        self.dma_hist = {"sp": [], "pool": []}
        self.bar = {e: [] for e in self.ENGS}
        self.waited = {e: {} for e in self.ENGS}
        self.all_dma = []

    def new_epoch(self):
        self.epoch += 1

    def _add(self, eng, fn, R, W, dma):
        op = Op(eng, fn, self.epoch, dma)
        op.idx = len(self.ops[eng])
        deps = []
        for b in R:
            if b.w is not None:
                deps.append(b.w)
        for b in W:
            if b.w is not None:
                deps.append(b.w)
            deps.extend(b.r.values())
        deps.extend(self.bar[eng])
        self.bar[eng] = []
        if dma:
            h = self.dma_hist[eng]
            op.dsem = len(h) % self.ND
            if len(h) >= self.ND:
                deps.append(h[-self.ND])
            op.dval = 16 * (len(h) // self.ND + 1)
            h.append(op)
            self.all_dma.append(op)
        best = {}
        wd = self.waited[eng]
        for d in deps:
            if d.dma:
                key = ("d", d.eng, d.dsem)
                cur = wd.get(key, 0)
                if d.dval <= cur:
                    continue
                if key not in best or best[key].dval < d.dval:
                    best[key] = d
            else:
                if d.eng == eng and eng in ("pe", "sp"):
                    continue
                key = ("c", d.eng, d.epoch)
                cur = wd.get(key, -1)
                if d.idx <= cur:
                    continue
                if key not in best or best[key].idx < d.idx:
                    best[key] = d
        for key, d in best.items():
            if d.dma:
                wd[key] = d.dval
            else:
                wd[key] = d.idx
                d.flag = True
            op.deps.append(d)
        for b in W:
            b.w = op
            b.r = {}
        for b in R:
            b.r[eng if not dma else ("dma", eng, op.dsem)] = op
        self.ops[eng].append(op)
        return op

    def op(self, eng, fn, R=(), W=()):
        return self._add(eng, fn, R, W, False)

    def dma(self, eng, fn, R=(), W=()):
        return self._add(eng, fn, R, W, True)

    def barrier(self):
        last = []
        for e in ("pe", "act", "dve", "pool"):
            if self.ops[e]:
                for o in reversed(self.ops[e]):
                    if not o.dma:
                        last.append(o)
                        break
        for q in ("sp", "pool"):
            h = self.dma_hist[q]
            last.extend(h[-self.ND:])
        for e in self.ENGS:
            self.bar[e] = list(last)

    def finalize(self, nc, es):
        self.csem = {}
        for e in ("pe", "act", "dve", "pool"):
            cnt = {}
            for o in self.ops[e]:
                if o.flag and not o.dma:
                    cnt[o.epoch] = cnt.get(o.epoch, 0) + 1
                    o.val = cnt[o.epoch]
                    key = (e, o.epoch)
                    if key not in self.csem:
                        self.csem[key] = es.enter_context(nc.semaphore(f"c_{e}_{o.epoch}"))
        self.dsems = {}
        for q in ("sp", "pool"):
            for i in range(self.ND):
                self.dsems[(q, i)] = es.enter_context(nc.semaphore(f"d_{q}_{i}"))

    def replay(self, eng, e):
        for o in self.ops[eng]:
            for d in o.deps:
                if d.dma:
                    e.wait_ge(self.dsems[(d.eng, d.dsem)], d.dval)
                else:
                    e.wait_ge(self.csem[(d.eng, d.epoch)], d.val)
            inst = o.fn(e)
            if o.dma:
                inst.then_inc(self.dsems[(o.eng, o.dsem)], 16)
            elif o.flag:
                inst.then_inc(self.csem[(o.eng, o.epoch)], 1)
        if eng == "sp":
            for q in ("sp", "pool"):
                h = self.dma_hist[q]
                for o in h[-self.ND:]:
                    e.wait_ge(self.dsems[(q, o.dsem)], o.dval)


def build(passes=(0, 1, 2, 3), heads=tuple(range(32)), groups=tuple(range(8)), do_B=True, do_C=True, dbg=False, stop=None, compact=False, info=None):
    nc = bass.Bass("TRN2", target_bir_lowering=False)

    def dr(name, shape, dt=F32, kind="ExternalInput"):
        return nc.dram_tensor(name, shape, dt, kind=kind).ap()

    xcat = dr("xcat", [4096, 4096])
    w_in = None
    w_out = dr("w_out", [8192, 4096] if do_B else [128, 128])
    gain_d = dr("gain_fm", [128, 32])
    lb0_d = dr("lb0_fm", [128, 32])
    lb1_d = dr("lb1_fm", [128, 32])
    rgain_d = dr("rgain_fm", [128, 32])
    sink_d = dr("sink_fm", [128, 32])
    post_d = dr("post_bc", [128, 4096])
    mprev_d = dr("mask_prev", [128, 128])
    mcur_d = dr("mask_cur", [128, 128])
    mprev0_d = dr("mask_prev0", [128, 128])
    m01_d = dr("mask01", [128, 128])
    out = dr("out", [2048, 4096], kind="ExternalOutput")
    mxs = nc.dram_tensor("mxs", [16, 128, 64, 128], BF16, kind=("ExternalOutput" if dbg else "Internal")).ap()
    ysc = nc.dram_tensor("ysc", [2048, 4096], F32, kind=("ExternalOutput" if dbg else "Internal")).ap()

    class StopBuild(Exception):
        pass

    def stage(name):
        if stop == name:
            raise StopBuild()

    P = Prog()
    es = ExitStack()
    E = es.enter_context
    AR_BYTES = 176128
    arena = E(nc.sbuf_tensor("arena", [128, AR_BYTES // 2], BF16))
    Sall = E(nc.sbuf_tensor("Sall", [128, 32, 128], F32))
    cst = E(nc.sbuf_tensor("cst", [128, 2048], F32))
    cbf = E(nc.sbuf_tensor("cbf", [128, 1024], BF16))
    pat = E(nc.sbuf_tensor("pat", [128, 1024], F32))
    psP = [E(nc.psum_tensor(f"psP{i}", [128, 512], F32)) for i in range(2)]
    psTb = [E(nc.psum_tensor(f"psT{i}", [128, 1024], BF16)) for i in range(2)]
    psG = [E(nc.psum_tensor(f"psG{i}", [128, 512], F32)) for i in range(4)]

    def view(off, nbytes, dt, shape=None):
        ap = arena[:, off // 2:(off + nbytes) // 2]
        if dt == F32:
            ap = ap.bitcast(F32)
        if shape is not None and len(shape) == 3:
            ap = ap.rearrange("p (a b) -> p a b", b=shape[2])
        return ap

    gain_fm = cst[:, 0:32]
    lb = cst[:, 32:64]
    oml = cst[:, 64:96]
    noml = cst[:, 96:128]
    rgain = cst[:, 128:160]
    esink = cst[:, 160:192]
    lb1 = cst[:, 192:224]
    epsT = cst[:, 224:225]
    ss = cst[:, 232:234]
    std = cst[:, 234:236]
    rstd = cst[:, 236:238]
    m01 = cst[:, 256:384]
    mtmp = cst[:, 384:768]
    ssq = cst[:, 768:896]
    rs2 = cst[:, 896:912]
    tmp16 = cst[:, 912:928]
    ident = cbf[:, 0:128]
    mk_prev = cbf[:, 128:256]
    mk_cur = cbf[:, 256:384]
    mk_prev0 = cbf[:, 384:512]
    onesE = cbf[:, 512:640]
    onesO = cbf[:, 640:768]
    onesD = cbf[:, 768:896]

    B_cst = Buf("cst")
    B_hT = Buf("hT")
    B_S = [Buf(f"S{j}") for j in range(32)]
    B_ring = [Buf(f"ring{i}") for i in range(NW)]
    B_ps = {nm: Buf(nm) for nm in ["psP0", "psP1", "psT0", "psT1", "psG0", "psG1", "psG2", "psG3"]}
    PSAP = {"psP0": psP[0], "psP1": psP[1], "psG0": psG[0], "psG1": psG[1], "psG2": psG[2], "psG3": psG[3]}

    def psb(nm, lo=0, hi=4):
        return [B_ps[nm]]

    B_mxs = [Buf(f"mxs{i}") for i in range(64)]
    B_ysc = [Buf(f"ysc{i}") for i in range(16)]
    B_out = Buf("out")
    B_small = Buf("small")

    hT = view(0, 73728, BF16, (128, 32, TP))
    ring = [view(73728 + i * 8192, 8192, BF16, (128, 32, 128)) for i in range(NW)]
    WK = 131072

    class Tl:
        def __init__(self, name, off, nbytes, dt, shape=None):
            self.ap = view(WK + off, nbytes, dt, shape)
            self.b = Buf(name)

    def setup():
        P.dma("sp", lambda e: e.dma_start(out=gain_fm, in_=gain_d), W=[B_cst])
        P.dma("sp", lambda e: e.dma_start(out=lb, in_=lb0_d), W=[B_cst])
        P.dma("sp", lambda e: e.dma_start(out=lb1, in_=lb1_d), W=[B_cst])
        P.dma("sp", lambda e: e.dma_start(out=rgain, in_=rgain_d), W=[B_cst])
        P.dma("sp", lambda e: e.dma_start(out=esink, in_=sink_d), W=[B_cst])
        P.dma("sp", lambda e: e.dma_start(out=m01, in_=m01_d), W=[B_cst])
        P.dma("sp", lambda e: e.dma_start(out=mtmp[:, 0:128], in_=mprev_d), W=[B_cst])
        P.dma("sp", lambda e: e.dma_start(out=mtmp[:, 128:256], in_=mcur_d), W=[B_cst])
        P.dma("sp", lambda e: e.dma_start(out=mtmp[:, 256:384], in_=mprev0_d), W=[B_cst])
        P.op("pool", lambda e: e.memset(pat[:], 1.0), W=[B_cst])
        P.op("pool", lambda e: e.memset(pat[:].rearrange("p (c t) -> p c t", t=64)[:, :, 0:1], 0.0), W=[B_cst])
        P.op("pool", lambda e: e.memset(cbf[:, 0:128], 1.0), W=[B_cst])
        P.op("pool", lambda e: e.affine_select(ident, ident, [[-1, 128]], ALU.is_equal, 0.0, base=0,
                                               channel_multiplier=1), W=[B_cst])
        P.op("pool", lambda e: e.memset(onesE[:, 0:64], 1.0), W=[B_cst])
        P.op("pool", lambda e: e.memset(onesE[:, 64:128], 0.0), W=[B_cst])
        P.op("pool", lambda e: e.memset(onesO[:, 0:64], 0.0), W=[B_cst])
        P.op("pool", lambda e: e.memset(onesO[:, 64:128], 1.0), W=[B_cst])
        P.op("pool", lambda e: e.memset(onesD, 1.0 / 128.0), W=[B_cst])
        P.op("pool", lambda e: e.memset(epsT, 1e-6), W=[B_cst])
        P.op("pool", lambda e: e.memset(Sall[:], 0.0), W=B_S)
        P.op("dve", lambda e: e.tensor_copy(cbf[:, 128:512], mtmp), R=[B_cst], W=[B_cst])
        P.op("dve", lambda e: e.tensor_tensor(tmp16.bitcast(F32) if False else cst[:, 1024:1056], lb, lb1, ALU.subtract),
             R=[B_cst], W=[B_cst])
        P.op("act", lambda e: e.activation(lb, cst[:, 1024:1056], AF.Sigmoid), R=[B_cst], W=[B_cst])
        P.op("dve", lambda e: e.tensor_scalar(oml, lb, -1.0, 1.0, ALU.mult, ALU.add), R=[B_cst], W=[B_cst])
        P.op("dve", lambda e: e.tensor_scalar(noml, lb, 1.0, -1.0, ALU.mult, ALU.add), R=[B_cst], W=[B_cst])
        P.op("act", lambda e: e.activation(esink, esink, AF.Exp), R=[B_cst], W=[B_cst])
        P.barrier()

    class WStream:
        def __init__(self, sched):
            self.sched = sched
            self.nload = 0
            self.ncons = 0
            for _ in range(min(NW, len(sched))):
                self._load()

        def _load(self):
            k = self.nload
            if k >= len(self.sched):
                return
            slot = k % NW
            for (col, n, dst) in self.sched[k]:
                src = w_in[:, col:col + n].rearrange("(kc p) c -> p kc c", p=128)
                dstap = ring[slot][:, :, dst:dst + n]
                P.dma("pool", lambda e, s=src, d=dstap: e.dma_start(out=d, in_=s), W=[B_ring[slot]])
            self.nload += 1

        def cur(self):
            slot = self.ncons % NW
            return ring[slot], B_ring[slot]

        def done(self):
            self.ncons += 1
            self._load()

    sched = []
    for p in passes:
        main = p >= 2
        for j in heads:
            if main:
                sched += [[(OFF["rf"] + j * 128, 128, 0)], [(OFF["ri"] + j * 128, 128, 0)],
                          [(OFF["rq"] + j * 128, 128, 0)], [(OFF["rg"] + j * 128, 128, 0)]]
            else:
                sched += [[(OFF["rf"] + j * 128, 128, 0)], [(OFF["ri"] + j * 128, 128, 0)]]
        if main:
            for a in groups:
                sched += [[(OFF["ak"] + a * 64, 64, 0), (OFF["ak"] + a * 64, 64, 64)],
                          [(OFF["av"] + a * 64, 64, 0)]]
                sched += [[(OFF["aq"] + a * 512 + i * 128, 128, 0)] for i in range(4)]
                sched += [[(OFF["ag"] + a * 512 + i * 128, 128, 0)] for i in range(4)]

    if compact:
        gather = []
        off = 0
        new = []
        for blk in sched:
            nb = []
            for (col, n, dst) in blk:
                gather.append((col, n))
                nb.append((off, n, dst))
                off += n
            new.append(nb)
        sched = new
        if info is not None:
            info["gather"] = gather
        w_in = dr("w_in", [4096, max(off, 128)])
    else:
        w_in = dr("w_in", [4096, 25600])

    evac_rr = [0]

    def proj(ws, M, toks, consume):
        wap, wb = ws.cur()
        for gi, (lo, n) in enumerate(toks):
            bank = evac_rr[0] % 2
            evac_rr[0] += 1
            ps = psP[bank]
            pb = psb(f"psP{bank}")
            for kc in range(32):
                P.op("pe", lambda e, ps=ps, kc=kc, lo=lo, n=n, wap=wap: e.matmul(
                    ps[0:M, 0:n], wap[:, kc, 0:M], hT[:, kc, lo:lo + n], start=(kc == 0), stop=(kc == 31)),
                    R=[wb, B_hT], W=pb)
            consume(ps, pb, gi, n)
        ws.done()

    XT = [Tl(f"xt{i}", i * 16384, 16384, F32) for i in range(2)]
    HN = Tl("hn", 32768, 8192, BF16)

    def norm_stage(p):
        main = p >= 2
        tiles = ([(p * 8 - 1, 0)] if main else []) + [(p * 8 + i, 1 + i) for i in range(8)]
        for k, (tidx, slot) in enumerate(tiles):
            xt = XT[k % 2]
            j = k % 2
            P.dma("sp", lambda e, xt=xt, tidx=tidx: e.dma_start(out=xt.ap, in_=xcat[tidx * 128:(tidx + 1) * 128, :]),
                  W=[xt.b])
            P.op("dve", lambda e, j=j: e.memset(ss[:, j:j + 1], 0.0), W=[B_small])
            P.op("act", lambda e, xt=xt, j=j: e.activation(HN.ap, xt.ap, AF.Square, accum_out=ss[:, j:j + 1]),
                 R=[xt.b], W=[HN.b, B_small])
            P.op("act", lambda e, j=j: e.activation(std[:, j:j + 1], ss[:, j:j + 1], AF.Sqrt, bias=epsT, scale=1.0 / 4096.0),
                 R=[B_small], W=[B_small])
            P.op("dve", lambda e, j=j: e.reciprocal(rstd[:, j:j + 1], std[:, j:j + 1]), R=[B_small], W=[B_small])
            P.op("dve", lambda e, xt=xt, j=j: e.tensor_scalar(HN.ap, xt.ap, rstd[:, j:j + 1], None, ALU.mult),
                 R=[xt.b, B_small], W=[HN.b])
            for g in range(8):
                half = g % 2
                pb = psb(f"psT{half}")
                for u in range(4):
                    kc = 4 * g + u
                    P.op("pe", lambda e, half=half, u=u, kc=kc: e.transpose(
                        psTb[half][:, u * 128:(u + 1) * 128], HN.ap[:, kc * 128:(kc + 1) * 128], ident),
                        R=[HN.b], W=pb)
                P.op("dve", lambda e, half=half, g=g, slot=slot: e.tensor_tensor(
                    hT[:, 4 * g:4 * g + 4, slot * 128:(slot + 1) * 128],
                    psTb[half][:, 0:512].rearrange("p (a b) -> p a b", b=128),
                    gain_fm[:, 4 * g:4 * g + 4].unsqueeze(2).to_broadcast([128, 4, 128]), ALU.mult),
                    R=pb, W=[B_hT])

    KB = 1024
    fA = Tl("fA", 0, 4 * KB, F32); fB = Tl("fB", 4 * KB, 4 * KB, F32)
    fC = Tl("fC", 8 * KB, 4 * KB, F32); fD = Tl("fD", 12 * KB, 4 * KB, F32)
    qs = Tl("qs", 16 * KB, 2 * KB, BF16); qt = Tl("qt", 18 * KB, 2 * KB, BF16)
    kt_ = Tl("kt", 20 * KB, 2 * KB, BF16); kh = Tl("kh", 22 * KB, 2 * KB, BF16)
    vT = Tl("vT", 24 * KB, 2 * KB, BF16); vtm = Tl("vtm", 26 * KB, 2 * KB, BF16, (128, 8, 128))
    khtm = Tl("khtm", 28 * KB, 2 * KB, BF16, (128, 8, 128)); Am = Tl("Am", 30 * KB, 2 * KB, BF16, (128, 8, 128))
    gs = Tl("gs", 32 * KB, 2 * KB, BF16); osq = Tl("osq", 34 * KB, 2 * KB, BF16)
    mx = Tl("mx", 36 * KB, 2 * KB, BF16); Sb = Tl("Sb", 38 * KB, 4 * KB, BF16, (128, 16, 128))
    Sf = [Tl("Sf0", 42 * KB, 512, F32), Tl("Sf1", 42 * KB + 512, 512, F32)]
    TOK2 = [(128, 512), (640, 512)]

    def evac_act(dst, func, scale=1.0):
        def c(ps, pb, gi, n):
            P.op("act", lambda e: e.activation(dst.ap[:, gi * 512:gi * 512 + n], ps[:, 0:n], func, scale=scale),
                 R=pb, W=[dst.b])
        return c

    def evac_copy(dst):
        def c(ps, pb, gi, n):
            P.op("dve", lambda e: e.tensor_copy(dst.ap[:, gi * 512:gi * 512 + n], ps[:, 0:n]), R=pb, W=[dst.b])
        return c

    def rnn_unit(ws, p, j):
        main = p >= 2
        proj(ws, 128, TOK2, evac_act(fA, AF.Sigmoid))
        P.op("dve", lambda e: e.tensor_scalar(fB.ap, fA.ap, oml[:, j:j + 1], lb[:, j:j + 1], ALU.mult, ALU.add),
             R=[fA.b], W=[fB.b])
        P.op("dve", lambda e: e.tensor_scalar(fC.ap, fA.ap, noml[:, j:j + 1], oml[:, j:j + 1], ALU.mult, ALU.add),
             R=[fA.b], W=[fC.b])
        P.op("act", lambda e: e.activation(fA.ap, fB.ap, AF.Ln), R=[fB.b], W=[fA.b])
        P.op("dve", lambda e: e.tensor_tensor_scan(fB.ap, pat[:], fA.ap, 0.0, ALU.mult, ALU.add),
             R=[fA.b], W=[fB.b])
        P.op("act", lambda e: e.activation(fA.ap, fB.ap, AF.Exp, scale=-1.0), R=[fB.b], W=[fA.b])
        P.op("act", lambda e: e.activation(fD.ap, fB.ap, AF.Exp), R=[fB.b], W=[fD.b])
        P.op("dve", lambda e: e.tensor_tensor(kt_.ap, fC.ap, fA.ap, ALU.mult), R=[fC.b, fA.b], W=[kt_.b])
        eG3 = fD.ap.rearrange("p (c t) -> p c t", t=64)
        P.op("dve", lambda e: e.tensor_tensor(kh.ap.rearrange("p (c t) -> p c t", t=64),
                                              kt_.ap.rearrange("p (c t) -> p c t", t=64),
                                              eG3[:, :, 63:64].to_broadcast([128, 16, 64]), ALU.mult),
             R=[kt_.b, fD.b], W=[kh.b])
        stage("rnn_a")
        proj(ws, 128, TOK2, evac_copy(vT))
        for src, dst in ((vT, vtm), (kh, khtm)):
            for g in range(2):
                pb = psb(f"psT{g}")
                for u in range(4):
                    i = 4 * g + u
                    P.op("pe", lambda e, g=g, u=u, i=i, src=src: e.transpose(
                        psTb[g][:, u * 128:(u + 1) * 128], src.ap[:, i * 128:(i + 1) * 128], ident),
                        R=[src.b], W=pb)
                P.op("dve", lambda e, g=g, dst=dst: e.tensor_copy(
                    dst.ap[:, 4 * g:4 * g + 4, :], psTb[g][:, 0:512].rearrange("p (a b) -> p a b", b=128)),
                    R=pb, W=[dst.b])
        stage("rnn_b")
        if main:
            proj(ws, 128, TOK2, evac_act(qs, AF.Silu))
            P.op("dve", lambda e: e.tensor_tensor(qt.ap, qs.ap, fD.ap, ALU.mult), R=[qs.b, fD.b], W=[qt.b])
            proj(ws, 128, TOK2, evac_act(gs, AF.Silu))
            for bk in range(2):
                bank = f"psG{bk}"
                for r in range(4):
                    i = bk * 4 + r
                    P.op("pe", lambda e, i=i, r=r, bk=bk: e.matmul(
                        psG[bk][:, r * 128:(r + 1) * 128], kt_.ap[:, i * 128:(i + 1) * 128],
                        qt.ap[:, i * 128:(i + 1) * 128], start=True, stop=True),
                        R=[kt_.b, qt.b], W=psb(bank))
                P.op("dve", lambda e, bk=bk: e.tensor_tensor(
                    Am.ap[:, bk * 4:bk * 4 + 4, :], psG[bk][:, :].rearrange("p (a b) -> p a b", b=128),
                    m01.unsqueeze(1).to_broadcast([128, 4, 128]), ALU.mult),
                    R=psb(bank), W=[Am.b])
        stage("rnn_c")
        ubank = {}
        for c in range(16):
            i, h = c // 2, c % 2
            if i < 4:
                bank = "psG2" if h == 0 else "psG3"
            else:
                bank = "psP0" if h == 0 else "psP1"
            r = i % 4
            ubank[c] = (bank, r)
            P.op("pe", lambda e, i=i, h=h, bank=bank, r=r: e.matmul(
                PSAP[bank][:, r * 128:(r + 1) * 128], khtm.ap[h * 64:(h + 1) * 64, i, :],
                vtm.ap[h * 64:(h + 1) * 64, i, :], start=True, stop=True),
                R=[khtm.b, vtm.b], W=psb(bank, r, r + 1))
        P.op("dve", lambda e: e.tensor_copy(Sf[0].ap, Sall[:, j, :]), R=[B_S[j]], W=[Sf[0].b])
        for c in range(16):
            cur, nxt = Sf[c % 2], Sf[(c + 1) % 2]
            bank, r = ubank[c]
            if main:
                P.op("act", lambda e, c=c, cur=cur: e.activation(Sb.ap[:, c, :], cur.ap, AF.Identity), R=[cur.b], W=[Sb.b])
            P.op("dve", lambda e, c=c, cur=cur, nxt=nxt, bank=bank, r=r: e.scalar_tensor_tensor(
                nxt.ap, cur.ap, fD.ap[:, c * 64 + 63:c * 64 + 64], PSAP[bank][:, r * 128:(r + 1) * 128],
                ALU.mult, ALU.add), R=[cur.b, fD.b] + psb(bank, r, r + 1), W=[nxt.b])
        P.op("dve", lambda e: e.tensor_copy(Sall[:, j, :], Sf[0].ap), R=[Sf[0].b], W=[B_S[j]])
        if not main:
            return
        stage("rnn_d")
        for i in range(8):
            bank = i // 4
            r = i % 4
            pb = psb(f"psG{bank}")
            reg = psG[bank][:, r * 128:(r + 1) * 128]
            P.op("pe", lambda e, i=i, reg=reg: e.matmul(reg, vtm.ap[:, i, :], Am.ap[:, i, :], start=True, stop=False),
                 R=[vtm.b, Am.b], W=pb)
            for h in range(2):
                c = 2 * i + h
                P.op("pe", lambda e, i=i, h=h, c=c, reg=reg: e.matmul(
                    reg[:, h * 64:(h + 1) * 64], Sb.ap[:, c, :], qt.ap[:, i * 128 + h * 64:i * 128 + (h + 1) * 64],
                    start=False, stop=(h == 1)), R=[Sb.b, qt.b], W=pb)
        stage("rnn_e")
        for g in range(2):
            pb = psb(f"psG{g}")
            P.op("dve", lambda e, g=g: e.tensor_copy(fA.ap[:, g * 512:(g + 1) * 512], psG[g][:, :]), R=pb, W=[fA.b])
            P.op("act", lambda e, g=g: e.activation(osq.ap[:, g * 512:(g + 1) * 512], fA.ap[:, g * 512:(g + 1) * 512],
                                                    AF.Square), R=[fA.b], W=[osq.b])
            mb = psb(f"psG{2 + g}")
            P.op("pe", lambda e, g=g: e.matmul(psG[2 + g][:, :], onesD, osq.ap[:, g * 512:(g + 1) * 512],
                                               start=True, stop=True), R=[osq.b], W=mb)
            P.op("act", lambda e, g=g: e.activation(fB.ap[:, g * 512:(g + 1) * 512], psG[2 + g][:, :], AF.Sqrt,
                                                    bias=epsT, scale=1.0), R=mb, W=[fB.b])
        P.op("dve", lambda e: e.reciprocal(fC.ap, fB.ap), R=[fB.b], W=[fC.b])
        P.op("dve", lambda e: e.tensor_tensor(fB.ap, fA.ap, fC.ap, ALU.mult), R=[fA.b, fC.b], W=[fB.b])
        P.op("dve", lambda e: e.scalar_tensor_tensor(mx.ap, fB.ap, rgain[:, j:j + 1], gs.ap, ALU.mult, ALU.mult),
             R=[fB.b, gs.b], W=[mx.b])
        tt0 = (p - 2) * 8
        P.dma("sp", lambda e: e.dma_start(
            out=mxs[tt0:tt0 + 8, :, 32 + j, :].rearrange("t p c -> p t c"),
            in_=mx.ap.rearrange("p (t c) -> p t c", c=128)), R=[mx.b], W=[B_mxs[32 + j]])

    KK = Tl("KK", 0, 2304, BF16); aVT = Tl("aVT", 2304, 2304, BF16)
    VE = Tl("VE", 4608, 2304, BF16, (128, 9, 128)); VO = Tl("VO", 6912, 2304, BF16, (128, 9, 128))
    QT = Tl("QT", 9216, 8192, BF16, (128, 4, 1024)); GT = Tl("GT", 17408, 8192, BF16, (128, 4, 1024))
    PT = Tl("PT", 25600, 4096, BF16, (128, 4, 512))
    dn = Tl("dn", 29696, 2048, F32); rc = Tl("rc", 31744, 2048, F32); at_ = Tl("at", 33792, 2048, F32)
    AO = Tl("AO", 35840, 8192, BF16, (128, 4, 1024))
    TOK3 = [(0, 128), (128, 512), (640, 512)]

    def attn_unit(ws, p, a):
        first = (p == 2)
        def c_kk(ps, pb, gi, n):
            lo = TOK3[gi][0]
            P.op("dve", lambda e: e.tensor_copy(KK.ap[:, lo:lo + n], ps[:, 0:n]), R=pb, W=[KK.b])
        proj(ws, 128, TOK3, c_kk)
        def c_v(ps, pb, gi, n):
            lo = TOK3[gi][0]
            P.op("act", lambda e: e.activation(aVT.ap[0:64, lo:lo + n], ps[0:64, 0:n], AF.Identity), R=pb, W=[aVT.b])
        proj(ws, 64, TOK3, c_v)
        for t0 in range(0, 9, 4):
            nt = min(4, 9 - t0)
            half = (t0 // 4) % 2
            pb = psb(f"psT{half}")
            for u in range(nt):
                tt = t0 + u
                P.op("pe", lambda e, half=half, u=u, tt=tt: e.transpose(
                    psTb[half][:, u * 64:(u + 1) * 64], aVT.ap[0:64, tt * 128:(tt + 1) * 128],
                    ident[0:64, 0:64]), R=[aVT.b], W=pb)
            src = psTb[half][:, 0:nt * 64].rearrange("p (a b) -> p a b", b=64)
            P.op("dve", lambda e, t0=t0, nt=nt, src=src: e.tensor_copy(VE.ap[:, t0:t0 + nt, 0:64], src), R=pb, W=[VE.b])
            P.op("dve", lambda e, t0=t0, nt=nt, src=src: e.tensor_copy(VO.ap[:, t0:t0 + nt, 64:128], src), R=pb, W=[VO.b])
        for i in range(4):
            def c_q(ps, pb, gi, n, i=i):
                P.op("act", lambda e: e.activation(QT.ap[:, i, gi * 512:gi * 512 + n], ps[:, 0:n], AF.Identity, scale=0.125),
                     R=pb, W=[QT.b])
            proj(ws, 128, TOK2, c_q)
        for i in range(4):
            def c_g(ps, pb, gi, n, i=i):
                P.op("act", lambda e: e.activation(GT.ap[:, i, gi * 512:gi * 512 + n], ps[:, 0:n], AF.Silu),
                     R=pb, W=[GT.b])
            proj(ws, 128, TOK2, c_g)
        for n in range(8):
            for kt in range(2):
                ktile = n + kt
                if kt == 0:
                    mk = mk_prev0 if (first and n == 0) else mk_prev
                else:
                    mk = mk_cur
                for par in range(2):
                    bi = kt * 2 + par
                    bank = f"psG{bi}"
                    pbk = psb(bank)
                    lo = 64 * par
                    for i in range(4):
                        reg = psG[bi][:, i * 128:(i + 1) * 128]
                        P.op("pe", lambda e, reg=reg, mk=mk: e.matmul(reg, ident, mk, start=True, stop=False),
                             R=[], W=pbk)
                        P.op("pe", lambda e, reg=reg, lo=lo, ktile=ktile, i=i, n=n: e.matmul(
                            reg, KK.ap[lo:lo + 64, ktile * 128:(ktile + 1) * 128],
                            QT.ap[lo:lo + 64, i, n * 128:(n + 1) * 128], start=False, stop=True),
                            R=[KK.b, QT.b], W=pbk)
                    P.op("act", lambda e, bi=bi: e.activation(PT.ap[:, bi, :], psG[bi][:, :], AF.Exp), R=pbk, W=[PT.b])
            pbo = psb("psP1")
            pbd = psb("psP0")
            seq = [(kt, par) for kt in range(2) for par in range(2)]
            for si, (kt, par) in enumerate(seq):
                ktile = n + kt
                vv = VE if par == 0 else VO
                P.op("pe", lambda e, vv=vv, ktile=ktile, kt=kt, par=par, si=si: e.matmul(
                    psP[1][:, :], vv.ap[:, ktile, :], PT.ap[:, kt * 2 + par, :], start=(si == 0), stop=(si == 3)),
                    R=[vv.b, PT.b], W=pbo)
            for si, (kt, par) in enumerate(seq):
                oo = onesE if par == 0 else onesO
                P.op("pe", lambda e, oo=oo, kt=kt, par=par, si=si: e.matmul(
                    psP[0][:, :], oo, PT.ap[:, kt * 2 + par, :], start=(si == 0), stop=(si == 3)),
                    R=[PT.b], W=pbd)
            P.op("dve", lambda e: e.tensor_tensor(
                dn.ap.rearrange("p (a b) -> p a b", b=128), psP[0][:, :].rearrange("p (a b) -> p a b", b=128),
                esink[:, a * 4:a * 4 + 4].unsqueeze(2).to_broadcast([128, 4, 128]), ALU.add), R=pbd, W=[dn.b])
            P.op("dve", lambda e: e.reciprocal(rc.ap, dn.ap), R=[dn.b], W=[rc.b])
            P.op("dve", lambda e: e.tensor_tensor(at_.ap, psP[1][:, :], rc.ap, ALU.mult), R=pbo + [rc.b], W=[at_.b])
            P.op("dve", lambda e, n=n: e.tensor_tensor(
                AO.ap[:, :, n * 128:(n + 1) * 128], at_.ap.rearrange("p (a b) -> p a b", b=128),
                GT.ap[:, :, n * 128:(n + 1) * 128], ALU.mult), R=[at_.b, GT.b], W=[AO.b])
        tt0 = (p - 2) * 8
        for i in range(4):
            P.dma("sp", lambda e, i=i: e.dma_start(
                out=mxs[tt0:tt0 + 8, :, 4 * a + i, :].rearrange("t p c -> p t c"),
                in_=AO.ap[:, i, :].rearrange("p (t c) -> p t c", c=128)), R=[AO.b], W=[B_mxs[4 * a + i]])

    def phaseA(ws):
      for p in passes:
        main = p >= 2
        norm_stage(p)
        stage("norm")
        P.barrier()
        for j in heads:
            rnn_unit(ws, p, j)
        if main:
            P.barrier()
            P.op("pool", lambda e: e.memset(VE.ap, 0.0), W=[VE.b])
            P.op("pool", lambda e: e.memset(VO.ap, 0.0), W=[VO.b])
            for a in groups:
                attn_unit(ws, p, a)
        P.barrier()
        P.new_epoch()

    try:
      setup()
      stage("setup")
      ws = WStream(sched)
      phaseA(ws)
    except StopBuild:
      do_B = False
      do_C = False

    P.barrier()
    P.new_epoch()


    WO = [view(i * 65536, 65536, BF16, (128, 64, 512)) for i in range(2)]
    B_WO = [Buf("wo0"), Buf("wo1")]
    MXT = [view(131072 + i * 16384, 16384, BF16, (128, 64, 128)) for i in range(2)]
    B_MXT = [Buf("mxt0"), Buf("mxt1")]
    YB = [view(163840 + i * 2048, 2048, F32) for i in range(2)]
    B_YB = [Buf("yb0"), Buf("yb1")]
    JK = view(167936, 2048, F32)
    B_JK = Buf("jk")

    def load_wo(cb):
        s = cb % 2
        for q in range(4):
            src = w_out[q * 2048:(q + 1) * 2048, cb * 512:(cb + 1) * 512].rearrange("(kc p) c -> p kc c", p=128)
            P.dma("pool", lambda e, src=src, s=s, q=q: e.dma_start(out=WO[s][:, q * 16:(q + 1) * 16, :], in_=src),
                  W=[B_WO[s]])

    P.op("dve", lambda e: e.memset(ssq, 0.0), W=[B_small])
    if do_B:
        load_wo(0)
        load_wo(1)
    k = 0
    for cb in (range(8) if do_B else []):
        s = cb % 2
        for tt in range(16):
            m = k % 2
            P.dma("sp", lambda e, m=m, tt=tt: e.dma_start(out=MXT[m], in_=mxs[tt]), R=B_mxs, W=[B_MXT[m]])
            bank = k % 2
            pb = psb(f"psP{bank}")
            for kc in range(64):
                P.op("pe", lambda e, bank=bank, m=m, kc=kc, s=s: e.matmul(
                    psP[bank][:, :], MXT[m][:, kc, :], WO[s][:, kc, :], start=(kc == 0), stop=(kc == 63)),
                    R=[B_MXT[m], B_WO[s]], W=pb)
            P.op("dve", lambda e, bank=bank, m=m: e.tensor_copy(YB[m], psP[bank][:, :]), R=pb, W=[B_YB[m]])
            P.op("act", lambda e, m=m, tt=tt, cb=cb: e.activation(
                JK, YB[m], AF.Square, accum_out=ssq[:, tt * 8 + cb:tt * 8 + cb + 1]), R=[B_YB[m]], W=[B_JK, B_small])
            P.dma("sp", lambda e, m=m, tt=tt, cb=cb: e.dma_start(
                out=ysc[tt * 128:(tt + 1) * 128, cb * 512:(cb + 1) * 512], in_=YB[m]), R=[B_YB[m]], W=[B_ysc[tt]])
            k += 1
        if cb + 2 < 8:
            load_wo(cb + 2)
    P.barrier()
    P.new_epoch()

    YR = [view(i * 16384, 16384, F32) for i in range(2)]
    XR = [view(32768 + i * 16384, 16384, F32) for i in range(2)]
    TR = [view(65536 + i * 16384, 16384, F32) for i in range(2)]
    PG = view(98304, 16384, F32)
    B_YR = [Buf("yr0"), Buf("yr1")]; B_XR = [Buf("xr0"), Buf("xr1")]; B_TR = [Buf("tr0"), Buf("tr1")]
    B_PG = Buf("pg")
    P.dma("sp", lambda e: e.dma_start(out=PG, in_=post_d), W=[B_PG])
    P.op("dve", lambda e: e.tensor_reduce(tmp16, ssq.rearrange("p (t c) -> p t c", c=8), mybir.AxisListType.X, ALU.add),
         R=[B_small], W=[B_small])
    P.op("act", lambda e: e.activation(tmp16, tmp16, AF.Sqrt, bias=epsT, scale=1.0 / 4096.0), R=[B_small], W=[B_small])
    P.op("dve", lambda e: e.reciprocal(rs2, tmp16), R=[B_small], W=[B_small])
    for tt in (range(16) if do_C else []):
        m = tt % 2
        P.dma("sp", lambda e, m=m, tt=tt: e.dma_start(out=YR[m], in_=ysc[tt * 128:(tt + 1) * 128, :]),
              R=[B_ysc[tt]], W=[B_YR[m]])
        P.dma("sp", lambda e, m=m, tt=tt: e.dma_start(out=XR[m], in_=xcat[2048 + tt * 128:2048 + (tt + 1) * 128, :]),
              W=[B_XR[m]])
        P.op("pool", lambda e, m=m: e.tensor_tensor(TR[m], YR[m], PG, ALU.mult), R=[B_YR[m], B_PG], W=[B_TR[m]])
        P.op("dve", lambda e, m=m, tt=tt: e.scalar_tensor_tensor(
            YR[m], TR[m], rs2[:, tt:tt + 1], XR[m], ALU.mult, ALU.add), R=[B_TR[m], B_XR[m], B_small], W=[B_YR[m]])
        P.dma("sp", lambda e, m=m, tt=tt: e.dma_start(out=out[tt * 128:(tt + 1) * 128, :], in_=YR[m]),
              R=[B_YR[m]], W=[])

    P.finalize(nc, es)
    block = E(nc.Block())

    @block.tensor
    def _(e):
        P.replay("pe", e)

    @block.scalar
    def _(e):
        P.replay("act", e)

    @block.vector
    def _(e):
        P.replay("dve", e)

    @block.gpsimd
    def _(e):
        P.replay("pool", e)

    @block.sync
    def _(e):
        P.replay("sp", e)

    es.close()
    return nc


def kernel(x, w_in, attn_sinks, lb_logits, rnn_norm, w_out, pre_norm, post_norm):
    x = np.asarray(x, np.float32)
    w_in2 = np.ascontiguousarray(np.asarray(w_in, np.float32)[0])
    w_out2 = np.ascontiguousarray(np.asarray(w_out, np.float32)[0])
    sinks = np.asarray(attn_sinks, np.float32)[0]
    lbl = np.asarray(lb_logits, np.float32)
    fm = lambda v: np.ascontiguousarray(np.asarray(v, np.float32).reshape(32, 128).T)
    pidx = np.arange(128)[:, None] >= 64
    cidx = np.arange(32)[None, :]
    hidx = (cidx // 4) * 8 + (cidx % 4) * 2 + pidx.astype(np.int64)
    sink_fm = np.ascontiguousarray(sinks[hidx])
    post_bc = np.ascontiguousarray(np.broadcast_to(np.asarray(post_norm, np.float32)[0][None, :], (128, 4096)))
    kk = np.arange(128)[:, None]
    qq = np.arange(128)[None, :]
    mask_prev = np.where(kk > qq, 0.0, NEG).astype(np.float32)
    mask_cur = np.where(kk <= qq, 0.0, NEG).astype(np.float32)
    mask_all = np.full((128, 128), NEG, np.float32)
    mask01 = ((kk <= qq) & ((kk // 64) == (qq // 64))).astype(np.float32)
    common = dict(w_in=w_in2, w_out=w_out2, gain_fm=fm(pre_norm[0]), lb0_fm=fm(lbl[0]), lb1_fm=fm(lbl[1]),
                  rgain_fm=fm(rnn_norm[0]), sink_fm=sink_fm, post_bc=post_bc, mask_prev=mask_prev,
                  mask_cur=mask_cur, mask01=mask01)
    in_maps = []
    for c in range(8):
        b, half = c // 2, c % 2
        xm = x[b, half * 2048:(half + 1) * 2048]
        xp = x[b, 0:2048] if half == 1 else np.zeros_like(xm)
        m = dict(common)
        m["xcat"] = np.ascontiguousarray(np.concatenate([xp, xm], axis=0))
        m["mask_prev0"] = mask_prev if half == 1 else mask_all
        in_maps.append(m)
    nc = build()
    res = run_bass_kernel_spmd(nc, in_maps, core_ids=list(range(8)))
    outp = np.empty((4, 4096, 4096), np.float32)
    for c in range(8):
        b, half = c // 2, c % 2
        outp[b, half * 2048:(half + 1) * 2048] = np.asarray(res.results[c]["out"])
    return outp
```

```python
import numpy as np
from contextlib import ExitStack
import concourse.bass as bass
import concourse.mybir as mybir
from concourse.bass_utils import run_bass_kernel_spmd

F32 = mybir.dt.float32
BF16 = mybir.dt.bfloat16
AF = mybir.ActivationFunctionType
ALU = mybir.AluOpType

NEG = -30000.0
NW = 7
T = 1024
TP = T + 128
OFF = dict(aq=0, ak=4096, av=4608, ag=5120, rq=9216, rf=13312, ri=17408, rg=21504)


class Buf:
    __slots__ = ("name", "w", "r")

    def __init__(self, name):
        self.name = name
        self.w = None
        self.r = {}


class Op:
    __slots__ = ("eng", "fn", "deps", "flag", "idx", "epoch", "dma", "dsem", "dval", "val")

    def __init__(self, eng, fn, epoch, dma=False):
        self.eng = eng
        self.fn = fn
        self.deps = []
        self.flag = False
        self.idx = -1
        self.epoch = epoch
        self.dma = dma
        self.dsem = None
        self.dval = 0
        self.val = 0


class Prog:
    ENGS = ("pe", "act", "dve", "pool", "sp")
    ND = 8

    def __init__(self):
        self.ops = {e: [] for e in self.ENGS}
        self.epoch = 0
        self.dma_hist = {"sp": [], "pool": []}
        self.bar = {e: [] for e in self.ENGS}
        self.waited = {e: {} for e in self.ENGS}
        self.all_dma = []

    def new_epoch(self):
        self.epoch += 1

    def _add(self, eng, fn, R, W, dma):
        op = Op(eng, fn, self.epoch, dma)
        op.idx = len(self.ops[eng])
        deps = []
        for b in R:
            if b.w is not None:
                deps.append(b.w)
        for b in W:
            if b.w is not None:
                deps.append(b.w)
            deps.extend(b.r.values())
        deps.extend(self.bar[eng])
        self.bar[eng] = []
        if dma:
            h = self.dma_hist[eng]
            op.dsem = len(h) % self.ND
            if len(h) >= self.ND:
                deps.append(h[-self.ND])
            op.dval = 16 * (len(h) // self.ND + 1)
            h.append(op)
            self.all_dma.append(op)
        best = {}
        wd = self.waited[eng]
        for d in deps:
            if d.dma:
                key = ("d", d.eng, d.dsem)
                cur = wd.get(key, 0)
                if d.dval <= cur:
                    continue
                if key not in best or best[key].dval < d.dval:
                    best[key] = d
            else:
                if d.eng == eng and eng in ("pe", "sp"):
                    continue
                key = ("c", d.eng, d.epoch)
                cur = wd.get(key, -1)
                if d.idx <= cur:
                    continue
                if key not in best or best[key].idx < d.idx:
                    best[key] = d
        for key, d in best.items():
            if d.dma:
                wd[key] = d.dval
            else:
                wd[key] = d.idx
                d.flag = True
            op.deps.append(d)
        for b in W:
            b.w = op
            b.r = {}
        for b in R:
            b.r[eng if not dma else ("dma", eng, op.dsem)] = op
        self.ops[eng].append(op)
        return op

    def op(self, eng, fn, R=(), W=()):
        return self._add(eng, fn, R, W, False)

    def dma(self, eng, fn, R=(), W=()):
        return self._add(eng, fn, R, W, True)

    def barrier(self):
        last = []
        for e in ("pe", "act", "dve", "pool"):
            if self.ops[e]:
                for o in reversed(self.ops[e]):
                    if not o.dma:
                        last.append(o)
                        break
        for q in ("sp", "pool"):
            h = self.dma_hist[q]
            last.extend(h[-self.ND:])
        for e in self.ENGS:
            self.bar[e] = list(last)

    def finalize(self, nc, es):
        self.csem = {}
        for e in ("pe", "act", "dve", "pool"):
            cnt = {}
            for o in self.ops[e]:
                if o.flag and not o.dma:
                    cnt[o.epoch] = cnt.get(o.epoch, 0) + 1
                    o.val = cnt[o.epoch]
                    key = (e, o.epoch)
                    if key not in self.csem:
                        self.csem[key] = es.enter_context(nc.semaphore(f"c_{e}_{o.epoch}"))
        self.dsems = {}
        for q in ("sp", "pool"):
            for i in range(self.ND):
                self.dsems[(q, i)] = es.enter_context(nc.semaphore(f"d_{q}_{i}"))

    def replay(self, eng, e):
        for o in self.ops[eng]:
            for d in o.deps:
                if d.dma:
                    e.wait_ge(self.dsems[(d.eng, d.dsem)], d.dval)
                else:
                    e.wait_ge(self.csem[(d.eng, d.epoch)], d.val)
            inst = o.fn(e)
            if o.dma:
                inst.then_inc(self.dsems[(o.eng, o.dsem)], 16)
            elif o.flag:
                inst.then_inc(self.csem[(o.eng, o.epoch)], 1)
        if eng == "sp":
            for q in ("sp", "pool"):
                h = self.dma_hist[q]
                for o in h[-self.ND:]:
                    e.wait_ge(self.dsems[(q, o.dsem)], o.dval)


def build(passes=(0, 1, 2, 3), heads=tuple(range(32)), groups=tuple(range(8)), do_B=True, do_C=True, dbg=False, stop=None, compact=False, info=None):
    nc = bass.Bass("TRN2", target_bir_lowering=False)

    def dr(name, shape, dt=F32, kind="ExternalInput"):
        return nc.dram_tensor(name, shape, dt, kind=kind).ap()

    xcat = dr("xcat", [4096, 4096])
    w_in = None
    w_out = dr("w_out", [8192, 4096] if do_B else [128, 128])
    gain_d = dr("gain_fm", [128, 32])
    lb0_d = dr("lb0_fm", [128, 32])
    lb1_d = dr("lb1_fm", [128, 32])
    rgain_d = dr("rgain_fm", [128, 32])
    sink_d = dr("sink_fm", [128, 32])
    post_d = dr("post_bc", [128, 4096])
    mprev_d = dr("mask_prev", [128, 128])
    mcur_d = dr("mask_cur", [128, 128])
    mprev0_d = dr("mask_prev0", [128, 128])
    m01_d = dr("mask01", [128, 128])
    out = dr("out", [2048, 4096], kind="ExternalOutput")
    mxs = nc.dram_tensor("mxs", [16, 128, 64, 128], BF16, kind=("ExternalOutput" if dbg else "Internal")).ap()
    ysc = nc.dram_tensor("ysc", [2048, 4096], F32, kind=("ExternalOutput" if dbg else "Internal")).ap()

    class StopBuild(Exception):
        pass

    def stage(name):
        if stop == name:
            raise StopBuild()

    P = Prog()
    es = ExitStack()
    E = es.enter_context
    AR_BYTES = 176128
    arena = E(nc.sbuf_tensor("arena", [128, AR_BYTES // 2], BF16))
    Sall = E(nc.sbuf_tensor("Sall", [128, 32, 128], F32))
    cst = E(nc.sbuf_tensor("cst", [128, 2048], F32))
    cbf = E(nc.sbuf_tensor("cbf", [128, 1024], BF16))
    pat = E(nc.sbuf_tensor("pat", [128, 1024], F32))
    psP = [E(nc.psum_tensor(f"psP{i}", [128, 512], F32)) for i in range(2)]
    psTb = [E(nc.psum_tensor(f"psT{i}", [128, 1024], BF16)) for i in range(2)]
    psG = [E(nc.psum_tensor(f"psG{i}", [128, 512], F32)) for i in range(4)]

    def view(off, nbytes, dt, shape=None):
        ap = arena[:, off // 2:(off + nbytes) // 2]
        if dt == F32:
            ap = ap.bitcast(F32)
        if shape is not None and len(shape) == 3:
            ap = ap.rearrange("p (a b) -> p a b", b=shape[2])
        return ap

    gain_fm = cst[:, 0:32]
    lb = cst[:, 32:64]
    oml = cst[:, 64:96]
    noml = cst[:, 96:128]
    rgain = cst[:, 128:160]
    esink = cst[:, 160:192]
    lb1 = cst[:, 192:224]
    epsT = cst[:, 224:225]
    ss = cst[:, 232:234]
    std = cst[:, 234:236]
    rstd = cst[:, 236:238]
    m01 = cst[:, 256:384]
    mtmp = cst[:, 384:768]
    ssq = cst[:, 768:896]
    rs2 = cst[:, 896:912]
    tmp16 = cst[:, 912:928]
    ident = cbf[:, 0:128]
    mk_prev = cbf[:, 128:256]
    mk_cur = cbf[:, 256:384]
    mk_prev0 = cbf[:, 384:512]
    onesE = cbf[:, 512:640]
    onesO = cbf[:, 640:768]
    onesD = cbf[:, 768:896]

    B_cst = Buf("cst")
    B_hT = Buf("hT")
    B_S = [Buf(f"S{j}") for j in range(32)]
    B_ring = [Buf(f"ring{i}") for i in range(NW)]
    B_ps = {nm: Buf(nm) for nm in ["psP0", "psP1", "psT0", "psT1", "psG0", "psG1", "psG2", "psG3"]}
    PSAP = {"psP0": psP[0], "psP1": psP[1], "psG0": psG[0], "psG1": psG[1], "psG2": psG[2], "psG3": psG[3],
            "psT0": psTb[0][:, :].bitcast(F32), "psT1": psTb[1][:, :].bitcast(F32)}

    def psb(nm, lo=0, hi=4):
        return [B_ps[nm]]

    B_mxs = [Buf(f"mxs{i}") for i in range(64)]
    B_ysc = [Buf(f"ysc{i}") for i in range(16)]
    B_out = Buf("out")
    B_small = Buf("small")

    hT = view(0, 73728, BF16, (128, 32, TP))
    ring = [view(73728 + i * 8192, 8192, BF16, (128, 32, 128)) for i in range(NW)]
    WK = 131072

    class Tl:
        def __init__(self, name, off, nbytes, dt, shape=None):
            self.ap = view(WK + off, nbytes, dt, shape)
            self.b = Buf(name)

    def setup():
        P.dma("sp", lambda e: e.dma_start(out=gain_fm, in_=gain_d), W=[B_cst])
        P.dma("sp", lambda e: e.dma_start(out=lb, in_=lb0_d), W=[B_cst])
        P.dma("sp", lambda e: e.dma_start(out=lb1, in_=lb1_d), W=[B_cst])
        P.dma("sp", lambda e: e.dma_start(out=rgain, in_=rgain_d), W=[B_cst])
        P.dma("sp", lambda e: e.dma_start(out=esink, in_=sink_d), W=[B_cst])
        P.dma("sp", lambda e: e.dma_start(out=m01, in_=m01_d), W=[B_cst])
        P.dma("sp", lambda e: e.dma_start(out=mtmp[:, 0:128], in_=mprev_d), W=[B_cst])
        P.dma("sp", lambda e: e.dma_start(out=mtmp[:, 128:256], in_=mcur_d), W=[B_cst])
        P.dma("sp", lambda e: e.dma_start(out=mtmp[:, 256:384], in_=mprev0_d), W=[B_cst])
        P.op("pool", lambda e: e.memset(pat[:], 1.0), W=[B_cst])
        P.op("pool", lambda e: e.memset(pat[:].rearrange("p (c t) -> p c t", t=64)[:, :, 0:1], 0.0), W=[B_cst])
        P.op("pool", lambda e: e.memset(cbf[:, 0:128], 1.0), W=[B_cst])
        P.op("pool", lambda e: e.affine_select(ident, ident, [[-1, 128]], ALU.is_equal, 0.0, base=0,
                                               channel_multiplier=1), W=[B_cst])
        P.op("pool", lambda e: e.memset(onesE[:, 0:64], 1.0), W=[B_cst])
        P.op("pool", lambda e: e.memset(onesE[:, 64:128], 0.0), W=[B_cst])
        P.op("pool", lambda e: e.memset(onesO[:, 0:64], 0.0), W=[B_cst])
        P.op("pool", lambda e: e.memset(onesO[:, 64:128], 1.0), W=[B_cst])
        P.op("pool", lambda e: e.memset(onesD, 1.0 / 128.0), W=[B_cst])
        P.op("pool", lambda e: e.memset(epsT, 1e-6), W=[B_cst])
        P.op("pool", lambda e: e.memset(Sall[:], 0.0), W=B_S)
        P.op("dve", lambda e: e.tensor_copy(cbf[:, 128:512], mtmp), R=[B_cst], W=[B_cst])
        P.op("dve", lambda e: e.tensor_tensor(tmp16.bitcast(F32) if False else cst[:, 1024:1056], lb, lb1, ALU.subtract),
             R=[B_cst], W=[B_cst])
        P.op("act", lambda e: e.activation(lb, cst[:, 1024:1056], AF.Sigmoid), R=[B_cst], W=[B_cst])
        P.op("dve", lambda e: e.tensor_scalar(oml, lb, -1.0, 1.0, ALU.mult, ALU.add), R=[B_cst], W=[B_cst])
        P.op("dve", lambda e: e.tensor_scalar(noml, lb, 1.0, -1.0, ALU.mult, ALU.add), R=[B_cst], W=[B_cst])
        P.op("act", lambda e: e.activation(esink, esink, AF.Exp), R=[B_cst], W=[B_cst])
        P.barrier()

    class WStream:
        def __init__(self, sched):
            self.sched = sched
            self.nload = 0
            self.ncons = 0
            for _ in range(min(NW, len(sched))):
                self._load()

        def _load(self):
            k = self.nload
            if k >= len(self.sched):
                return
            slot = k % NW
            for (col, n, dst) in self.sched[k]:
                src = w_in[:, col:col + n].rearrange("(kc p) c -> p kc c", p=128)
                dstap = ring[slot][:, :, dst:dst + n]
                P.dma("pool", lambda e, s=src, d=dstap: e.dma_start(out=d, in_=s), W=[B_ring[slot]])
            self.nload += 1

        def cur(self):
            slot = self.ncons % NW
            return ring[slot], B_ring[slot]

        def done(self):
            self.ncons += 1
            self._load()

    sched = []
    for p in passes:
        main = p >= 2
        for j in heads:
            if main:
                sched += [[(OFF["rf"] + j * 128, 128, 0)], [(OFF["ri"] + j * 128, 128, 0)],
                          [(OFF["rq"] + j * 128, 128, 0)], [(OFF["rg"] + j * 128, 128, 0)]]
            else:
                sched += [[(OFF["rf"] + j * 128, 128, 0)], [(OFF["ri"] + j * 128, 128, 0)]]
        if main:
            for a in groups:
                sched += [[(OFF["ak"] + a * 64, 64, 0), (OFF["ak"] + a * 64, 64, 64)],
                          [(OFF["av"] + a * 64, 64, 0)]]
                sched += [[(OFF["aq"] + a * 512 + i * 128, 128, 0)] for i in range(4)]
                sched += [[(OFF["ag"] + a * 512 + i * 128, 128, 0)] for i in range(4)]

    if compact:
        gather = []
        off = 0
        new = []
        for blk in sched:
            nb = []
            for (col, n, dst) in blk:
                gather.append((col, n))
                nb.append((off, n, dst))
                off += n
            new.append(nb)
        sched = new
        if info is not None:
            info["gather"] = gather
        w_in = dr("w_in", [4096, max(off, 128)])
    else:
        w_in = dr("w_in", [4096, 25600])

    evac_rr = [0]

    def proj(ws, M, toks, consume):
        wap, wb = ws.cur()
        for gi, (lo, n) in enumerate(toks):
            bank = evac_rr[0] % 2
            evac_rr[0] += 1
            ps = psP[bank]
            pb = psb(f"psP{bank}")
            for kc in range(32):
                P.op("pe", lambda e, ps=ps, kc=kc, lo=lo, n=n, wap=wap: e.matmul(
                    ps[0:M, 0:n], wap[:, kc, 0:M], hT[:, kc, lo:lo + n], start=(kc == 0), stop=(kc == 31)),
                    R=[wb, B_hT], W=pb)
            consume(ps, pb, gi, n)
        ws.done()

    XT = [Tl(f"xt{i}", i * 16384, 16384, F32) for i in range(2)]
    HN = Tl("hn", 32768, 8192, BF16)

    def norm_stage(p):
        main = p >= 2
        tiles = ([(p * 8 - 1, 0)] if main else []) + [(p * 8 + i, 1 + i) for i in range(8)]
        for k, (tidx, slot) in enumerate(tiles):
            xt = XT[k % 2]
            j = k % 2
            P.dma("sp", lambda e, xt=xt, tidx=tidx: e.dma_start(out=xt.ap, in_=xcat[tidx * 128:(tidx + 1) * 128, :]),
                  W=[xt.b])
            P.op("dve", lambda e, j=j: e.memset(ss[:, j:j + 1], 0.0), W=[B_small])
            P.op("act", lambda e, xt=xt, j=j: e.activation(HN.ap, xt.ap, AF.Square, accum_out=ss[:, j:j + 1]),
                 R=[xt.b], W=[HN.b, B_small])
            P.op("act", lambda e, j=j: e.activation(std[:, j:j + 1], ss[:, j:j + 1], AF.Sqrt, bias=epsT, scale=1.0 / 4096.0),
                 R=[B_small], W=[B_small])
            P.op("dve", lambda e, j=j: e.reciprocal(rstd[:, j:j + 1], std[:, j:j + 1]), R=[B_small], W=[B_small])
            P.op("dve", lambda e, xt=xt, j=j: e.tensor_scalar(HN.ap, xt.ap, rstd[:, j:j + 1], None, ALU.mult),
                 R=[xt.b, B_small], W=[HN.b])
            for g in range(8):
                half = g % 2
                pb = psb(f"psT{half}")
                for u in range(4):
                    kc = 4 * g + u
                    P.op("pe", lambda e, half=half, u=u, kc=kc: e.transpose(
                        psTb[half][:, u * 128:(u + 1) * 128], HN.ap[:, kc * 128:(kc + 1) * 128], ident),
                        R=[HN.b], W=pb)
                P.op("dve", lambda e, half=half, g=g, slot=slot: e.tensor_tensor(
                    hT[:, 4 * g:4 * g + 4, slot * 128:(slot + 1) * 128],
                    psTb[half][:, 0:512].rearrange("p (a b) -> p a b", b=128),
                    gain_fm[:, 4 * g:4 * g + 4].unsqueeze(2).to_broadcast([128, 4, 128]), ALU.mult),
                    R=pb, W=[B_hT])

    KB = 1024
    fA = Tl("fA", 0, 4 * KB, F32); fB = Tl("fB", 4 * KB, 4 * KB, F32)
    fC = Tl("fC", 8 * KB, 4 * KB, F32); fD = Tl("fD", 12 * KB, 4 * KB, F32)
    qs = Tl("qs", 16 * KB, 2 * KB, BF16); qt = Tl("qt", 18 * KB, 2 * KB, BF16)
    kt_ = Tl("kt", 20 * KB, 2 * KB, BF16); kh = Tl("kh", 22 * KB, 2 * KB, BF16)
    vT = Tl("vT", 24 * KB, 2 * KB, BF16); vtm = Tl("vtm", 26 * KB, 2 * KB, BF16, (128, 8, 128))
    khtm = Tl("khtm", 28 * KB, 2 * KB, BF16, (128, 8, 128)); Am = Tl("Am", 30 * KB, 2 * KB, BF16, (128, 8, 128))
    gs = Tl("gs", 32 * KB, 2 * KB, BF16); osq = Tl("osq", 34 * KB, 2 * KB, BF16)
    mx = Tl("mx", 36 * KB, 2 * KB, BF16); Sb = Tl("Sb", 38 * KB, 4 * KB, BF16, (128, 16, 128))
    Sf = [Tl("Sf0", 42 * KB, 512, F32), Tl("Sf1", 42 * KB + 512, 512, F32)]
    TOK2 = [(128, 512), (640, 512)]

    def evac_act(dst, func, scale=1.0):
        def c(ps, pb, gi, n):
            P.op("act", lambda e: e.activation(dst.ap[:, gi * 512:gi * 512 + n], ps[:, 0:n], func, scale=scale),
                 R=pb, W=[dst.b])
        return c

    def evac_copy(dst):
        def c(ps, pb, gi, n):
            P.op("dve", lambda e: e.tensor_copy(dst.ap[:, gi * 512:gi * 512 + n], ps[:, 0:n]), R=pb, W=[dst.b])
        return c

    sg_ap = view(WK + 22 * KB, 4 * KB, F32)
    sg_b = [kh.b, vT.b]

    def rnn_head(ws, p, j):
        def c(ps, pb, gi, n):
            P.op("act", lambda e: e.activation(sg_ap[:, gi * 512:gi * 512 + n], ps[:, 0:n], AF.Sigmoid),
                 R=pb, W=sg_b)
        proj(ws, 128, TOK2, c)

    def rnn_unit(ws, p, j, nxt_j=None):
        main = p >= 2
        P.op("dve", lambda e: e.tensor_scalar(fB.ap, sg_ap, oml[:, j:j + 1], lb[:, j:j + 1], ALU.mult, ALU.add),
             R=sg_b, W=[fB.b])
        P.op("dve", lambda e: e.tensor_scalar(fC.ap, sg_ap, noml[:, j:j + 1], oml[:, j:j + 1], ALU.mult, ALU.add),
             R=sg_b, W=[fC.b])
        P.op("act", lambda e: e.activation(fA.ap, fB.ap, AF.Ln), R=[fB.b], W=[fA.b])
        P.op("dve", lambda e: e.tensor_tensor_scan(fB.ap, pat[:], fA.ap, 0.0, ALU.mult, ALU.add),
             R=[fA.b], W=[fB.b])
        P.op("act", lambda e: e.activation(fA.ap, fB.ap, AF.Exp, scale=-1.0), R=[fB.b], W=[fA.b])
        P.op("act", lambda e: e.activation(fD.ap, fB.ap, AF.Exp), R=[fB.b], W=[fD.b])
        P.op("dve", lambda e: e.tensor_tensor(kt_.ap, fC.ap, fA.ap, ALU.mult), R=[fC.b, fA.b], W=[kt_.b])
        eG3 = fD.ap.rearrange("p (c t) -> p c t", t=64)
        P.op("dve", lambda e: e.tensor_tensor(kh.ap.rearrange("p (c t) -> p c t", t=64),
                                              kt_.ap.rearrange("p (c t) -> p c t", t=64),
                                              eG3[:, :, 63:64].to_broadcast([128, 16, 64]), ALU.mult),
             R=[kt_.b, fD.b], W=[kh.b])
        stage("rnn_a")
        proj(ws, 128, TOK2, evac_copy(vT))
        for src, dst in ((vT, vtm), (kh, khtm)):
            for g in range(2):
                pb = psb(f"psT{g}")
                for u in range(4):
                    i = 4 * g + u
                    P.op("pe", lambda e, g=g, u=u, i=i, src=src: e.transpose(
                        psTb[g][:, u * 128:(u + 1) * 128], src.ap[:, i * 128:(i + 1) * 128], ident),
                        R=[src.b], W=pb)
                P.op("dve", lambda e, g=g, dst=dst: e.tensor_copy(
                    dst.ap[:, 4 * g:4 * g + 4, :], psTb[g][:, 0:512].rearrange("p (a b) -> p a b", b=128)),
                    R=pb, W=[dst.b])
        stage("rnn_b")
        if main:
            proj(ws, 128, TOK2, evac_act(qs, AF.Silu))
            P.op("dve", lambda e: e.tensor_tensor(qt.ap, qs.ap, fD.ap, ALU.mult), R=[qs.b, fD.b], W=[qt.b])
            proj(ws, 128, TOK2, evac_act(gs, AF.Silu))
            for bk in range(2):
                bank = f"psG{bk}"
                for r in range(4):
                    i = bk * 4 + r
                    P.op("pe", lambda e, i=i, r=r, bk=bk: e.matmul(
                        psG[bk][:, r * 128:(r + 1) * 128], kt_.ap[:, i * 128:(i + 1) * 128],
                        qt.ap[:, i * 128:(i + 1) * 128], start=True, stop=True),
                        R=[kt_.b, qt.b], W=psb(bank))
                P.op("dve", lambda e, bk=bk: e.tensor_tensor(
                    Am.ap[:, bk * 4:bk * 4 + 4, :], psG[bk][:, :].rearrange("p (a b) -> p a b", b=128),
                    m01.unsqueeze(1).to_broadcast([128, 4, 128]), ALU.mult),
                    R=psb(bank), W=[Am.b])
        stage("rnn_c")
        ubank = {}
        for c in range(16):
            i, h = c // 2, c % 2
            if i < 4:
                bank = "psG2" if h == 0 else "psG3"
            else:
                bank = "psT0" if h == 0 else "psT1"
            r = i % 4
            ubank[c] = (bank, r)
            P.op("pe", lambda e, i=i, h=h, bank=bank, r=r: e.matmul(
                PSAP[bank][:, r * 128:(r + 1) * 128], khtm.ap[h * 64:(h + 1) * 64, i, :],
                vtm.ap[h * 64:(h + 1) * 64, i, :], start=True, stop=True),
                R=[khtm.b, vtm.b], W=psb(bank, r, r + 1))
        P.op("dve", lambda e: e.tensor_copy(Sf[0].ap, Sall[:, j, :]), R=[B_S[j]], W=[Sf[0].b])
        for c in range(16):
            cur, nxt = Sf[c % 2], Sf[(c + 1) % 2]
            bank, r = ubank[c]
            if main:
                P.op("act", lambda e, c=c, cur=cur: e.activation(Sb.ap[:, c, :], cur.ap, AF.Identity), R=[cur.b], W=[Sb.b])
            P.op("dve", lambda e, c=c, cur=cur, nxt=nxt, bank=bank, r=r: e.scalar_tensor_tensor(
                nxt.ap, cur.ap, fD.ap[:, c * 64 + 63:c * 64 + 64], PSAP[bank][:, r * 128:(r + 1) * 128],
                ALU.mult, ALU.add), R=[cur.b, fD.b] + psb(bank, r, r + 1), W=[nxt.b])
        P.op("dve", lambda e: e.tensor_copy(Sall[:, j, :], Sf[0].ap), R=[Sf[0].b], W=[B_S[j]])
        if nxt_j is not None:
            rnn_head(ws, p, nxt_j)
        if not main:
            return
        stage("rnn_d")
        for i in range(8):
            bank = i // 4
            r = i % 4
            pb = psb(f"psG{bank}")
            reg = psG[bank][:, r * 128:(r + 1) * 128]
            P.op("pe", lambda e, i=i, reg=reg: e.matmul(reg, vtm.ap[:, i, :], Am.ap[:, i, :], start=True, stop=False),
                 R=[vtm.b, Am.b], W=pb)
            for h in range(2):
                c = 2 * i + h
                P.op("pe", lambda e, i=i, h=h, c=c, reg=reg: e.matmul(
                    reg[:, h * 64:(h + 1) * 64], Sb.ap[:, c, :], qt.ap[:, i * 128 + h * 64:i * 128 + (h + 1) * 64],
                    start=False, stop=(h == 1)), R=[Sb.b, qt.b], W=pb)
        stage("rnn_e")
        for g in range(2):
            pb = psb(f"psG{g}")
            P.op("dve", lambda e, g=g: e.tensor_copy(fA.ap[:, g * 512:(g + 1) * 512], psG[g][:, :]), R=pb, W=[fA.b])
            P.op("act", lambda e, g=g: e.activation(osq.ap[:, g * 512:(g + 1) * 512], fA.ap[:, g * 512:(g + 1) * 512],
                                                    AF.Square), R=[fA.b], W=[osq.b])
            mb = psb(f"psG{2 + g}")
            P.op("pe", lambda e, g=g: e.matmul(psG[2 + g][:, :], onesD, osq.ap[:, g * 512:(g + 1) * 512],
                                               start=True, stop=True), R=[osq.b], W=mb)
            P.op("act", lambda e, g=g: e.activation(fB.ap[:, g * 512:(g + 1) * 512], psG[2 + g][:, :], AF.Sqrt,
                                                    bias=epsT, scale=1.0), R=mb, W=[fB.b])
        P.op("dve", lambda e: e.reciprocal(fC.ap, fB.ap), R=[fB.b], W=[fC.b])
        P.op("dve", lambda e: e.tensor_tensor(fB.ap, fA.ap, fC.ap, ALU.mult), R=[fA.b, fC.b], W=[fB.b])
        P.op("dve", lambda e: e.scalar_tensor_tensor(mx.ap, fB.ap, rgain[:, j:j + 1], gs.ap, ALU.mult, ALU.mult),
             R=[fB.b, gs.b], W=[mx.b])
        tt0 = (p - 2) * 8
        P.dma("sp", lambda e: e.dma_start(
            out=mxs[tt0:tt0 + 8, :, 32 + j, :].rearrange("t p c -> p t c"),
            in_=mx.ap.rearrange("p (t c) -> p t c", c=128)), R=[mx.b], W=[B_mxs[32 + j]])

    KK = Tl("KK", 0, 2304, BF16); aVT = Tl("aVT", 2304, 2304, BF16)
    VE = Tl("VE", 4608, 2304, BF16, (128, 9, 128)); VO = Tl("VO", 6912, 2304, BF16, (128, 9, 128))
    QT = Tl("QT", 9216, 8192, BF16, (128, 4, 1024)); GT = Tl("GT", 17408, 8192, BF16, (128, 4, 1024))
    PT = Tl("PT", 25600, 4096, BF16, (128, 4, 512))
    dn = Tl("dn", 29696, 2048, F32); rc = Tl("rc", 31744, 2048, F32); at_ = Tl("at", 33792, 2048, F32)
    AO = Tl("AO", 35840, 8192, BF16, (128, 4, 1024))
    TOK3 = [(0, 128), (128, 512), (640, 512)]

    def attn_unit(ws, p, a):
        first = (p == 2)
        def c_kk(ps, pb, gi, n):
            lo = TOK3[gi][0]
            P.op("dve", lambda e: e.tensor_copy(KK.ap[:, lo:lo + n], ps[:, 0:n]), R=pb, W=[KK.b])
        proj(ws, 128, TOK3, c_kk)
        def c_v(ps, pb, gi, n):
            lo = TOK3[gi][0]
            P.op("act", lambda e: e.activation(aVT.ap[0:64, lo:lo + n], ps[0:64, 0:n], AF.Identity), R=pb, W=[aVT.b])
        proj(ws, 64, TOK3, c_v)
        for t0 in range(0, 9, 4):
            nt = min(4, 9 - t0)
            half = (t0 // 4) % 2
            pb = psb(f"psT{half}")
            for u in range(nt):
                tt = t0 + u
                P.op("pe", lambda e, half=half, u=u, tt=tt: e.transpose(
                    psTb[half][:, u * 64:(u + 1) * 64], aVT.ap[0:64, tt * 128:(tt + 1) * 128],
                    ident[0:64, 0:64]), R=[aVT.b], W=pb)
            src = psTb[half][:, 0:nt * 64].rearrange("p (a b) -> p a b", b=64)
            P.op("dve", lambda e, t0=t0, nt=nt, src=src: e.tensor_copy(VE.ap[:, t0:t0 + nt, 0:64], src), R=pb, W=[VE.b])
            P.op("dve", lambda e, t0=t0, nt=nt, src=src: e.tensor_copy(VO.ap[:, t0:t0 + nt, 64:128], src), R=pb, W=[VO.b])
        for i in range(4):
            def c_q(ps, pb, gi, n, i=i):
                P.op("act", lambda e: e.activation(QT.ap[:, i, gi * 512:gi * 512 + n], ps[:, 0:n], AF.Identity, scale=0.125),
                     R=pb, W=[QT.b])
            proj(ws, 128, TOK2, c_q)
        for i in range(4):
            def c_g(ps, pb, gi, n, i=i):
                P.op("act", lambda e: e.activation(GT.ap[:, i, gi * 512:gi * 512 + n], ps[:, 0:n], AF.Silu),
                     R=pb, W=[GT.b])
            proj(ws, 128, TOK2, c_g)
        for n in range(8):
            for kt in range(2):
                ktile = n + kt
                if kt == 0:
                    mk = mk_prev0 if (first and n == 0) else mk_prev
                else:
                    mk = mk_cur
                for par in range(2):
                    bi = kt * 2 + par
                    bank = f"psG{bi}"
                    pbk = psb(bank)
                    lo = 64 * par
                    for i in range(4):
                        reg = psG[bi][:, i * 128:(i + 1) * 128]
                        P.op("pe", lambda e, reg=reg, mk=mk: e.matmul(reg, ident, mk, start=True, stop=False),
                             R=[], W=pbk)
                        P.op("pe", lambda e, reg=reg, lo=lo, ktile=ktile, i=i, n=n: e.matmul(
                            reg, KK.ap[lo:lo + 64, ktile * 128:(ktile + 1) * 128],
                            QT.ap[lo:lo + 64, i, n * 128:(n + 1) * 128], start=False, stop=True),
                            R=[KK.b, QT.b], W=pbk)
                    P.op("act", lambda e, bi=bi: e.activation(PT.ap[:, bi, :], psG[bi][:, :], AF.Exp), R=pbk, W=[PT.b])
            pbo = psb("psP1")
            pbd = psb("psP0")
            seq = [(kt, par) for kt in range(2) for par in range(2)]
            for si, (kt, par) in enumerate(seq):
                ktile = n + kt
                vv = VE if par == 0 else VO
                P.op("pe", lambda e, vv=vv, ktile=ktile, kt=kt, par=par, si=si: e.matmul(
                    psP[1][:, :], vv.ap[:, ktile, :], PT.ap[:, kt * 2 + par, :], start=(si == 0), stop=(si == 3)),
                    R=[vv.b, PT.b], W=pbo)
            for si, (kt, par) in enumerate(seq):
                oo = onesE if par == 0 else onesO
                P.op("pe", lambda e, oo=oo, kt=kt, par=par, si=si: e.matmul(
                    psP[0][:, :], oo, PT.ap[:, kt * 2 + par, :], start=(si == 0), stop=(si == 3)),
                    R=[PT.b], W=pbd)
            P.op("dve", lambda e: e.tensor_tensor(
                dn.ap.rearrange("p (a b) -> p a b", b=128), psP[0][:, :].rearrange("p (a b) -> p a b", b=128),
                esink[:, a * 4:a * 4 + 4].unsqueeze(2).to_broadcast([128, 4, 128]), ALU.add), R=pbd, W=[dn.b])
            P.op("dve", lambda e: e.reciprocal(rc.ap, dn.ap), R=[dn.b], W=[rc.b])
            P.op("dve", lambda e: e.tensor_tensor(at_.ap, psP[1][:, :], rc.ap, ALU.mult), R=pbo + [rc.b], W=[at_.b])
            P.op("dve", lambda e, n=n: e.tensor_tensor(
                AO.ap[:, :, n * 128:(n + 1) * 128], at_.ap.rearrange("p (a b) -> p a b", b=128),
                GT.ap[:, :, n * 128:(n + 1) * 128], ALU.mult), R=[at_.b, GT.b], W=[AO.b])
        tt0 = (p - 2) * 8
        for i in range(4):
            P.dma("sp", lambda e, i=i: e.dma_start(
                out=mxs[tt0:tt0 + 8, :, 4 * a + i, :].rearrange("t p c -> p t c"),
                in_=AO.ap[:, i, :].rearrange("p (t c) -> p t c", c=128)), R=[AO.b], W=[B_mxs[4 * a + i]])

    def phaseA(ws):
      for p in passes:
        main = p >= 2
        norm_stage(p)
        stage("norm")
        P.barrier()
        hl = list(heads)
        if hl:
            rnn_head(ws, p, hl[0])
        for ji, j in enumerate(hl):
            rnn_unit(ws, p, j, hl[ji + 1] if ji + 1 < len(hl) else None)
        if main:
            P.barrier()
            P.op("pool", lambda e: e.memset(VE.ap, 0.0), W=[VE.b])
            P.op("pool", lambda e: e.memset(VO.ap, 0.0), W=[VO.b])
            for a in groups:
                attn_unit(ws, p, a)
        P.barrier()
        P.new_epoch()

    try:
      setup()
      stage("setup")
      ws = WStream(sched)
      phaseA(ws)
    except StopBuild:
      do_B = False
      do_C = False

    P.barrier()
    P.new_epoch()


    WO = [view(i * 65536, 65536, BF16, (128, 64, 512)) for i in range(2)]
    B_WO = [Buf("wo0"), Buf("wo1")]
    MXT = [view(131072 + i * 16384, 16384, BF16, (128, 64, 128)) for i in range(2)]
    MXT.append(Sall[:].rearrange("p a b -> p (a b)").bitcast(BF16).rearrange("p (a b) -> p a b", b=128))
    B_MXT = [Buf("mxt0"), Buf("mxt1"), Buf("mxt2")]
    YB = [view(163840 + i * 2048, 2048, F32) for i in range(2)]
    B_YB = [Buf("yb0"), Buf("yb1")]
    JK = view(167936, 2048, F32)
    B_JK = Buf("jk")

    def load_wo(cb):
        s = cb % 2
        for q in range(4):
            src = w_out[q * 2048:(q + 1) * 2048, cb * 512:(cb + 1) * 512].rearrange("(kc p) c -> p kc c", p=128)
            P.dma("pool", lambda e, src=src, s=s, q=q: e.dma_start(out=WO[s][:, q * 16:(q + 1) * 16, :], in_=src),
                  W=[B_WO[s]])

    P.op("dve", lambda e: e.memset(ssq, 0.0), W=[B_small])
    if do_B:
        load_wo(0)
        load_wo(1)
    k = 0
    NB = 8 * 16 if do_B else 0

    def load_mxt(kk):
        if kk < NB:
            P.dma("sp", lambda e, m=kk % 3, tt=kk % 16: e.dma_start(out=MXT[m], in_=mxs[tt]), R=B_mxs, W=[B_MXT[kk % 3]])

    load_mxt(0)
    load_mxt(1)
    for cb in (range(8) if do_B else []):
        s = cb % 2
        for tt in range(16):
            m = k % 3
            load_mxt(k + 2)
            bank = k % 2
            pb = psb(f"psP{bank}")
            for kc in range(64):
                P.op("pe", lambda e, bank=bank, m=m, kc=kc, s=s: e.matmul(
                    psP[bank][:, :], MXT[m][:, kc, :], WO[s][:, kc, :], start=(kc == 0), stop=(kc == 63)),
                    R=[B_MXT[m], B_WO[s]], W=pb)
            P.op("dve", lambda e, bank=bank: e.tensor_copy(YB[bank], psP[bank][:, :]), R=pb, W=[B_YB[bank]])
            P.op("act", lambda e, bank=bank, tt=tt, cb=cb: e.activation(
                JK, YB[bank], AF.Square, accum_out=ssq[:, tt * 8 + cb:tt * 8 + cb + 1]), R=[B_YB[bank]], W=[B_JK, B_small])
            P.dma("sp", lambda e, bank=bank, tt=tt, cb=cb: e.dma_start(
                out=ysc[tt * 128:(tt + 1) * 128, cb * 512:(cb + 1) * 512], in_=YB[bank]), R=[B_YB[bank]], W=[B_ysc[tt]])
            k += 1
        if cb + 2 < 8:
            load_wo(cb + 2)
    P.barrier()
    P.new_epoch()

    YR = [view(i * 16384, 16384, F32) for i in range(2)]
    XR = [view(32768 + i * 16384, 16384, F32) for i in range(2)]
    TR = [view(65536 + i * 16384, 16384, F32) for i in range(2)]
    PG = view(98304, 16384, F32)
    B_YR = [Buf("yr0"), Buf("yr1")]; B_XR = [Buf("xr0"), Buf("xr1")]; B_TR = [Buf("tr0"), Buf("tr1")]
    B_PG = Buf("pg")
    P.dma("sp", lambda e: e.dma_start(out=PG, in_=post_d), W=[B_PG])
    P.op("dve", lambda e: e.tensor_reduce(tmp16, ssq.rearrange("p (t c) -> p t c", c=8), mybir.AxisListType.X, ALU.add),
         R=[B_small], W=[B_small])
    P.op("act", lambda e: e.activation(tmp16, tmp16, AF.Sqrt, bias=epsT, scale=1.0 / 4096.0), R=[B_small], W=[B_small])
    P.op("dve", lambda e: e.reciprocal(rs2, tmp16), R=[B_small], W=[B_small])
    for tt in (range(16) if do_C else []):
        m = tt % 2
        P.dma("sp", lambda e, m=m, tt=tt: e.dma_start(out=YR[m], in_=ysc[tt * 128:(tt + 1) * 128, :]),
              R=[B_ysc[tt]], W=[B_YR[m]])
        P.dma("sp", lambda e, m=m, tt=tt: e.dma_start(out=XR[m], in_=xcat[2048 + tt * 128:2048 + (tt + 1) * 128, :]),
              W=[B_XR[m]])
        P.op("pool", lambda e, m=m: e.tensor_tensor(TR[m], YR[m], PG, ALU.mult), R=[B_YR[m], B_PG], W=[B_TR[m]])
        P.op("dve", lambda e, m=m, tt=tt: e.scalar_tensor_tensor(
            YR[m], TR[m], rs2[:, tt:tt + 1], XR[m], ALU.mult, ALU.add), R=[B_TR[m], B_XR[m], B_small], W=[B_YR[m]])
        P.dma("sp", lambda e, m=m, tt=tt: e.dma_start(out=out[tt * 128:(tt + 1) * 128, :], in_=YR[m]),
              R=[B_YR[m]], W=[])

    P.finalize(nc, es)
    block = E(nc.Block())

    @block.tensor
    def _(e):
        P.replay("pe", e)

    @block.scalar
    def _(e):
        P.replay("act", e)

    @block.vector
    def _(e):
        P.replay("dve", e)

    @block.gpsimd
    def _(e):
        P.replay("pool", e)

    @block.sync
    def _(e):
        P.replay("sp", e)

    es.close()
    return nc


def kernel(x, w_in, attn_sinks, lb_logits, rnn_norm, w_out, pre_norm, post_norm):
    x = np.asarray(x, np.float32)
    w_in2 = np.ascontiguousarray(np.asarray(w_in, np.float32)[0])
    w_out2 = np.ascontiguousarray(np.asarray(w_out, np.float32)[0])
    sinks = np.asarray(attn_sinks, np.float32)[0]
    lbl = np.asarray(lb_logits, np.float32)
    fm = lambda v: np.ascontiguousarray(np.asarray(v, np.float32).reshape(32, 128).T)
    pidx = np.arange(128)[:, None] >= 64
    cidx = np.arange(32)[None, :]
    hidx = (cidx // 4) * 8 + (cidx % 4) * 2 + pidx.astype(np.int64)
    sink_fm = np.ascontiguousarray(sinks[hidx])
    post_bc = np.ascontiguousarray(np.broadcast_to(np.asarray(post_norm, np.float32)[0][None, :], (128, 4096)))
    kk = np.arange(128)[:, None]
    qq = np.arange(128)[None, :]
    mask_prev = np.where(kk > qq, 0.0, NEG).astype(np.float32)
    mask_cur = np.where(kk <= qq, 0.0, NEG).astype(np.float32)
    mask_all = np.full((128, 128), NEG, np.float32)
    mask01 = ((kk <= qq) & ((kk // 64) == (qq // 64))).astype(np.float32)
    common = dict(w_in=w_in2, w_out=w_out2, gain_fm=fm(pre_norm[0]), lb0_fm=fm(lbl[0]), lb1_fm=fm(lbl[1]),
                  rgain_fm=fm(rnn_norm[0]), sink_fm=sink_fm, post_bc=post_bc, mask_prev=mask_prev,
                  mask_cur=mask_cur, mask01=mask01)
    in_maps = []
    for c in range(8):
        b, half = c // 2, c % 2
        xm = x[b, half * 2048:(half + 1) * 2048]
        xp = x[b, 0:2048] if half == 1 else np.zeros_like(xm)
        m = dict(common)
        m["xcat"] = np.ascontiguousarray(np.concatenate([xp, xm], axis=0))
        m["mask_prev0"] = mask_prev if half == 1 else mask_all
        in_maps.append(m)
    nc = build()
    res = run_bass_kernel_spmd(nc, in_maps, core_ids=list(range(8)))
    outp = np.empty((4, 4096, 4096), np.float32)
    for c in range(8):
        b, half = c // 2, c % 2
        outp[b, half * 2048:(half + 1) * 2048] = np.asarray(res.results[c]["out"])
    return outp
```
